# Optimizing a Trainium2 kernel written in Bass

```python
import math
import jax
import jax.numpy as jnp
from jax import lax
import numpy as np

D_MODEL = 2048
BATCH = 32
SEQ = 256
DEPTH = 2
DEC_BATCH = 2
DEC_SEQ = 1024
PAST_LEN = 512

GRID_W = 64
N_HEADS = 16
HEAD_DIM = D_MODEL // N_HEADS
NA_KH = 8
NA_KW = 16
Q_BLOCK = 128
ROW_BLOCK = Q_BLOCK // GRID_W
GROUP_CH = 16
N_GROUPS = D_MODEL // GROUP_CH
SSM_STATE = 64
D_FF = 128 * ((8 * D_MODEL // 3 + 127) // 128)
N_ATTN = (DEPTH + 1) // 2
N_SSM = DEPTH // 2
NORM_EPS = 1e-6
NEG_INF = -1e30

kernel_name = 'hybrid_na_s5_prefix_diffusion_step'


def rms_norm(x, g):
    xf = x.astype(jnp.float32)
    y = xf * lax.rsqrt(jnp.mean(xf * xf, axis=-1, keepdims=True) + NORM_EPS)
    return (y * g.astype(jnp.float32)).astype(x.dtype)


def ada_params(cond, w, b):
    mod = jax.nn.silu(cond) @ w + b
    return jnp.split(mod[:, None, :], 6, axis=-1)


def split_heads(t):
    return t.reshape(t.shape[:-1] + (N_HEADS, HEAD_DIM))


def dense_attention(q, k, v):
    bsz, length = q.shape[:2]
    nb = length // Q_BLOCK
    qb = q.reshape(bsz, nb, Q_BLOCK, N_HEADS, HEAD_DIM).swapaxes(0, 1)

    def one_block(qi):
        s = jnp.einsum('bqhd,bkhd->bhqk', qi, k).astype(jnp.float32) * HEAD_DIM ** -0.5
        p = jax.nn.softmax(s, axis=-1).astype(v.dtype)
        return jnp.einsum('bhqk,bkhd->bqhd', p, v)

    o = lax.map(one_block, qb)
    return o.swapaxes(0, 1).reshape(bsz, length, D_MODEL)


def na_context(h, w_qkv, w_o):
    q, k, v = (split_heads(t) for t in jnp.split(h @ w_qkv, 3, axis=-1))
    return dense_attention(q, k, v) @ w_o, k, v


def na_latent(h, k_ctx, v_ctx, w_qkv, rpb, w_o):
    bsz, length, _ = h.shape
    rows = length // GRID_W
    kh = min(NA_KH, rows)
    nrb = rows // ROW_BLOCK
    q, k, v = (split_heads(t).reshape(bsz, rows, GRID_W, N_HEADS, HEAD_DIM)
               for t in jnp.split(h @ w_qkv, 3, axis=-1))
    r = jnp.arange(rows)
    row_idx = jnp.clip(r - kh // 2, 0, rows - kh)[:, None] + jnp.arange(kh)[None, :]
    drow = row_idx - r[:, None]
    col = jnp.arange(GRID_W)
    col_start = jnp.clip(col - NA_KW // 2, 0, GRID_W - NA_KW)
    col_in = (col[None, :] >= col_start[:, None]) & (col[None, :] < col_start[:, None] + NA_KW)
    dcol = jnp.clip(col[None, :] - col[:, None], -(NA_KW - 1), NA_KW - 1) + NA_KW - 1
    col_bias = jnp.where(col_in, rpb.astype(jnp.float32)[:, :, dcol], NEG_INF)
    scale = HEAD_DIM ** -0.5
    n_win = kh * GRID_W

    def one_block(args):
        q_b, ridx, dr = args
        k_rows = k[:, ridx]
        v_rows = v[:, ridx]
        bias = col_bias[:, dr + NA_KH - 1].transpose(0, 1, 3, 2, 4)
        s_win = jnp.einsum('brqhd,brjkhd->bhrqjk', q_b, k_rows).astype(jnp.float32) * scale + bias
        s_ctx = jnp.einsum('brqhd,bkhd->bhrqk', q_b, k_ctx).astype(jnp.float32) * scale
        logits = jnp.concatenate([s_win.reshape(s_win.shape[:4] + (n_win,)), s_ctx], axis=-1)
        p = jax.nn.softmax(logits, axis=-1).astype(v.dtype)
        p_win = p[..., :n_win].reshape(s_win.shape)
        return (jnp.einsum('bhrqjk,brjkhd->brqhd', p_win, v_rows)
                + jnp.einsum('bhrqk,bkhd->brqhd', p[..., n_win:], v_ctx))

    q_blocks = q.reshape(bsz, nrb, ROW_BLOCK, GRID_W, N_HEADS, HEAD_DIM).swapaxes(0, 1)
    o = lax.map(one_block, (q_blocks, row_idx.reshape(nrb, ROW_BLOCK, kh), drow.reshape(nrb, ROW_BLOCK, kh)))
    return o.swapaxes(0, 1).reshape(bsz, length, D_MODEL) @ w_o


def _ssm_combine(e1, e2):
    a1r, a1i, b1r, b1i = e1
    a2r, a2i, b2r, b2i = e2
    return (a1r * a2r - a1i * a2i, a1r * a2i + a1i * a2r,
            a2r * b1r - a2i * b1i + b2r, a2r * b1i + a2i * b1r + b2i)


def s5_direction(ug, h0_re, h0_im, a_re, a_im, log_dt, b_re, b_im, c_re, c_im, reverse):
    length = ug.shape[1]
    dt = jnp.exp(log_dt)[:, None]
    mag = jnp.exp(a_re * dt)
    abar_re = mag * jnp.cos(a_im * dt)
    abar_im = mag * jnp.sin(a_im * dt)
    den = a_re * a_re + a_im * a_im
    num_re = abar_re - 1.0
    coef_re = (num_re * a_re + abar_im * a_im) / den
    coef_im = (abar_im * a_re - num_re * a_im) / den
    bb_re = coef_re[..., None] * b_re - coef_im[..., None] * b_im
    bb_im = coef_re[..., None] * b_im + coef_im[..., None] * b_re
    bu_re = jnp.einsum('blgc,gpc->blgp', ug, bb_re)
    bu_im = jnp.einsum('blgc,gpc->blgp', ug, bb_im)
    edge = length - 1 if reverse else 0
    bu_re = bu_re.at[:, edge].add(abar_re * h0_re - abar_im * h0_im)
    bu_im = bu_im.at[:, edge].add(abar_re * h0_im + abar_im * h0_re)
    a_shape = (1, length) + abar_re.shape
    _, _, hs_re, hs_im = lax.associative_scan(
        _ssm_combine,
        (jnp.broadcast_to(abar_re, a_shape), jnp.broadcast_to(abar_im, a_shape), bu_re, bu_im),
        reverse=reverse, axis=1)
    y = jnp.einsum('blgp,gcp->blgc', hs_re, c_re) - jnp.einsum('blgp,gcp->blgc', hs_im, c_im)
    last = 0 if reverse else length - 1
    return y, hs_re[:, last], hs_im[:, last]


def s5_mixer(h, h0_re, h0_im, w_in, a_re, a_im, log_dt, b_re, b_im, c_re, c_im, d_skip, w_gate, w_out):
    bsz, length, _ = h.shape
    f32 = lambda t: t.astype(jnp.float32)
    u = f32(h @ w_in)
    ug = u.reshape(bsz, length, N_GROUPS, GROUP_CH)
    y = f32(d_skip) * u
    fin_re, fin_im = [], []
    for d, rev in ((0, False), (1, True)):
        y_d, s_re, s_im = s5_direction(ug, f32(h0_re[:, d]), f32(h0_im[:, d]), f32(a_re[d]), f32(a_im[d]),
                                       f32(log_dt[d]), f32(b_re[d]), f32(b_im[d]), f32(c_re[d]),
                                       f32(c_im[d]), rev)
        y = y + y_d.reshape(bsz, length, D_MODEL)
        fin_re.append(s_re)
        fin_im.append(s_im)
    y = jax.nn.gelu(y).astype(h.dtype)
    y = y * jax.nn.sigmoid(y @ w_gate)
    return y @ w_out, jnp.stack(fin_re, axis=1), jnp.stack(fin_im, axis=1)


def conv_ffn(h, w_up, conv_w, conv_b, w_down):
    length = h.shape[1]
    up = h @ w_up
    pad = jnp.pad(up, ((0, 0), (1, 1), (0, 0)))
    z = conv_w[0] * pad[:, :length] + conv_w[1] * pad[:, 1:length + 1] + conv_w[2] * pad[:, 2:] + conv_b
    gate, val = jnp.split(z, 2, axis=-1)
    return (jax.nn.silu(gate) * val) @ w_down


def setup_inputs(seed: int = 0) -> dict:
    key = jax.random.key(seed)
    ks = iter(jax.random.split(key, 40))
    nrm = lambda shape, s: jax.random.normal(next(ks), shape, jnp.float32) * s
    D, F, G, P, GC = D_MODEL, D_FF, N_GROUPS, SSM_STATE, GROUP_CH
    n_idx = jnp.arange(P, dtype=jnp.float32)
    return {
        'x_prompt': nrm((BATCH, SEQ, D), 1.0),
        'x_sample': nrm((DEC_BATCH, DEC_SEQ, D), 1.0),
        'cache_k': nrm((DEC_BATCH, N_ATTN, PAST_LEN, N_HEADS, HEAD_DIM), 1.0),
        'cache_v': nrm((DEC_BATCH, N_ATTN, PAST_LEN, N_HEADS, HEAD_DIM), 1.0),
        'state_re': nrm((DEC_BATCH, N_SSM, 2, G, P), 0.5),
        'state_im': nrm((DEC_BATCH, N_SSM, 2, G, P), 0.5),
        'c': nrm((DEC_BATCH, D), 1.0),
        'c_ctx': nrm((D,), 1.0),
        'ada_w': nrm((DEPTH, D, 6 * D), 0.5 * D ** -0.5),
        'ada_b': nrm((DEPTH, 6 * D), 0.02),
        'norm1_g': 1.0 + nrm((DEPTH, D), 0.02),
        'norm2_g': 1.0 + nrm((DEPTH, D), 0.02),
        'na_w_qkv': nrm((N_ATTN, D, 3 * D), D ** -0.5),
        'na_rpb': nrm((N_ATTN, N_HEADS, 2 * NA_KH - 1, 2 * NA_KW - 1), 0.1),
        'na_w_o': nrm((N_ATTN, D, D), D ** -0.5),
        's5_w_in': nrm((N_SSM, D, D), D ** -0.5),
        's5_a_re': -0.5 + nrm((N_SSM, 2, G, P), 0.01),
        's5_a_im': math.pi * n_idx + nrm((N_SSM, 2, G, P), 0.01),
        's5_log_dt': jax.random.uniform(next(ks), (N_SSM, 2, G), jnp.float32, math.log(1e-3), math.log(1e-1)),
        's5_b_re': nrm((N_SSM, 2, G, P, GC), (2 * GC) ** -0.5),
        's5_b_im': nrm((N_SSM, 2, G, P, GC), (2 * GC) ** -0.5),
        's5_c_re': nrm((N_SSM, 2, G, GC, P), (2 * P) ** -0.5),
        's5_c_im': nrm((N_SSM, 2, G, GC, P), (2 * P) ** -0.5),
        's5_d': nrm((N_SSM, D), 1.0),
        's5_w_gate': nrm((N_SSM, D, D), D ** -0.5),
        's5_w_out': nrm((N_SSM, D, D), D ** -0.5),
        'ffn_w_up': nrm((DEPTH, D, 2 * F), D ** -0.5),
        'ffn_conv_w': nrm((DEPTH, 3, 2 * F), 3 ** -0.5),
        'ffn_conv_b': nrm((DEPTH, 2 * F), 0.02),
        'ffn_w_down': nrm((DEPTH, F, D), F ** -0.5),
        'final_g': 1.0 + nrm((D,), 0.02),
    }


def reference(x_prompt, x_sample, cache_k, cache_v, state_re, state_im, c, c_ctx,
              ada_w, ada_b, norm1_g, norm2_g, na_w_qkv, na_rpb, na_w_o,
              s5_w_in, s5_a_re, s5_a_im, s5_log_dt, s5_b_re, s5_b_im, s5_c_re, s5_c_im, s5_d,
              s5_w_gate, s5_w_out, ffn_w_up, ffn_conv_w, ffn_conv_b, ffn_w_down, final_g):
    xp, xs = x_prompt, x_sample
    new_k, new_v, new_sre, new_sim = [], [], [], []
    for i in range(DEPTH):
        j = i // 2
        mp = ada_params(c_ctx[None, :], ada_w[i], ada_b[i])
        ms = ada_params(c, ada_w[i], ada_b[i])
        hp = rms_norm(xp, norm1_g[i]) * (1 + mp[1]) + mp[0]
        hs = rms_norm(xs, norm1_g[i]) * (1 + ms[1]) + ms[0]
        if i % 2 == 0:
            op, kc, vc = na_context(hp, na_w_qkv[j], na_w_o[j])
            os_ = na_latent(hs, cache_k[:, j], cache_v[:, j], na_w_qkv[j], na_rpb[j], na_w_o[j])
            new_k.append(kc)
            new_v.append(vc)
        else:
            ssm = (s5_w_in[j], s5_a_re[j], s5_a_im[j], s5_log_dt[j], s5_b_re[j], s5_b_im[j],
                   s5_c_re[j], s5_c_im[j], s5_d[j], s5_w_gate[j], s5_w_out[j])
            zero_state = jnp.zeros((xp.shape[0], 2, N_GROUPS, SSM_STATE), jnp.float32)
            op, s_re, s_im = s5_mixer(hp, zero_state, zero_state, *ssm)
            os_, _, _ = s5_mixer(hs, state_re[:, j], state_im[:, j], *ssm)
            new_sre.append(s_re)
            new_sim.append(s_im)
        xp = xp + mp[2] * op
        xs = xs + ms[2] * os_
        ffn = (ffn_w_up[i], ffn_conv_w[i], ffn_conv_b[i], ffn_w_down[i])
        hp = rms_norm(xp, norm2_g[i]) * (1 + mp[4]) + mp[3]
        hs = rms_norm(xs, norm2_g[i]) * (1 + ms[4]) + ms[3]
        xp = xp + mp[5] * conv_ffn(hp, *ffn)
        xs = xs + ms[5] * conv_ffn(hs, *ffn)
    return (rms_norm(xp, final_g), rms_norm(xs, final_g),
            jnp.stack(new_k, axis=1), jnp.stack(new_v, axis=1),
            jnp.stack(new_sre, axis=1), jnp.stack(new_sim, axis=1))
```

```python
import numpy as np
from contextlib import ExitStack
import concourse.bass as bass
import concourse.mybir as mybir
from concourse.bass_utils import run_bass_kernel_spmd

F32 = mybir.dt.float32
BF16 = mybir.dt.bfloat16
AF = mybir.ActivationFunctionType
ALU = mybir.AluOpType

D = 2048
NCH = 16
SEQ = 256
NH = 16
FF = 5504
FCH = 43
NT = 512
NCORES = 8
EPS = 1e-6
ATT_SCALE = 128.0 ** -0.5


class KB:
    def __init__(self, nc, es):
        self.nc = nc
        self.es = es
        self.ops = {e: [] for e in ("pe", "act", "dve", "pool", "sp")}
        self.sems = {}
        self.cnt = {}
        self.lastw = {}
        self.reads = {}
        self.seen = {e: {} for e in self.ops}
        for e in self.ops:
            self.newsem("E_" + e)

    def newsem(self, name):
        s = self.es.enter_context(self.nc.semaphore(name))
        self.sems[name] = s
        self.cnt[name] = 0
        return name

    def _need(self, eng, waits, ev):
        if ev is None:
            return
        sn, val = ev
        if eng == "pe" and sn == "E_pe":
            return
        if self.seen[eng].get(sn, 0) >= val:
            return
        if waits.get(sn, 0) < val:
            waits[sn] = val

    def op(self, eng, fn, reads=(), writes=(), dma_sem=None, inc=None):
        waits = {}
        for k in reads:
            self._need(eng, waits, self.lastw.get(k))
        for k in writes:
            self._need(eng, waits, self.lastw.get(k))
            for sn, val in self.reads.get(k, {}).items():
                self._need(eng, waits, (sn, val))
        for sn, val in waits.items():
            self.seen[eng][sn] = val
        if dma_sem is None:
            sn = "E_" + eng
            self.cnt[sn] += 1
            ev = (sn, self.cnt[sn])
            inc = 1
        else:
            inc = 16 if inc is None else inc
            self.cnt[dma_sem] += inc
            ev = (dma_sem, self.cnt[dma_sem])
        for k in writes:
            self.lastw[k] = ev
            self.reads[k] = {}
        for k in reads:
            r = self.reads.setdefault(k, {})
            if r.get(ev[0], 0) < ev[1]:
                r[ev[0]] = ev[1]
        self.ops[eng].append((list(waits.items()), fn, ev[0], inc))
        return ev

    def barrier(self):
        evs = [(sn, c) for sn, c in self.cnt.items() if c > 0]
        for eng in self.ops:
            waits = {}
            for ev in evs:
                self._need(eng, waits, ev)
            for sn, val in waits.items():
                self.seen[eng][sn] = val
            self.ops[eng].append((list(waits.items()), None, None, 0))
        self.lastw = {}
        self.reads = {}

    def emit(self, block, final_waits):
        kb = self

        def run(eng_name, e):
            for waits, fn, sn, inc in kb.ops[eng_name]:
                for wsn, val in waits:
                    e.wait_ge(kb.sems[wsn], val)
                if fn is not None:
                    fn(e).then_inc(kb.sems[sn], inc)

        @block.tensor
        def _(e):
            run("pe", e)

        @block.scalar
        def _(e):
            run("act", e)

        @block.vector
        def _(e):
            run("dve", e)

        @block.gpsimd
        def _(e):
            run("pool", e)

        @block.sync
        def _(e):
            run("sp", e)
            for sn, val in final_waits:
                e.wait_ge(kb.sems[sn], val)


def build_program():
    nc = bass.Bass("TRN2", target_bir_lowering=False)
    dt = lambda name, shape, kind="ExternalInput": nc.dram_tensor(name, shape, F32, kind=kind).ap()
    xp = dt("xp", [4 * SEQ, D])
    cond = dt("cond", [2, D])
    ada_w = dt("ada_w", [2, D, 6 * D])
    ada_b = dt("ada_b", [2, 6 * D])
    norm1_g = dt("norm1_g", [2, D])
    norm2_g = dt("norm2_g", [2, D])
    w_qkv = dt("na_w_qkv", [D, 3 * D])
    w_o = dt("na_w_o", [D, D])
    ffn_up = dt("ffn_w_up", [2, D, 2 * FF])
    ffn_cw = dt("ffn_conv_w", [2, 3, 2 * FF])
    ffn_cb = dt("ffn_conv_b", [2, 2 * FF])
    ffn_dn = dt("ffn_w_down", [2, FF, D])
    final_g = dt("final_g", [D])
    s5_w_in = dt("s5_w_in", [D, D])
    s5_w_gate = dt("s5_w_gate", [D, D])
    s5_w_out = dt("s5_w_out", [D, D])
    s5_a_re = dt("s5_a_re", [2, 128, 64])
    s5_a_im = dt("s5_a_im", [2, 128, 64])
    s5_log_dt = dt("s5_log_dt", [2, 128])
    s5_b_re = dt("s5_b_re", [2, 128, 64, 16])
    s5_b_im = dt("s5_b_im", [2, 128, 64, 16])
    s5_c_re = dt("s5_c_re", [2, 128, 16, 64])
    s5_c_im = dt("s5_c_im", [2, 128, 16, 64])
    s5_d = dt("s5_d", [D])
    xsw = dt("xsw", [768, D])
    cache_k_in = dt("cache_k", [512, NH, 128])
    cache_v_in = dt("cache_v", [512, NH, 128])
    rpbp_t = nc.dram_tensor("rpbp", [7568], F32, kind="ExternalInput")
    st_h0 = [dt("state_re", [2, 2, 128, 64]), dt("state_im", [2, 2, 128, 64])]
    cm_in = dt("cm_in", [64, 2, 64])
    j_in = dt("j_in", [64, 64])
    rowm_in = dt("rowm_in", [128, 24])
    oh_in = dt("oh_in", [128, 24])
    halo_in = [dt(f"halo_in{i}", [128, 32], "Internal") for i in range(2)]
    halo_out = [dt(f"halo_out{i}", [NCORES * 128, 32], "Internal") for i in range(2)]
    st_in = dt("st_in", [128, 256], "Internal")
    st_out = dt("st_out", [NCORES * 128, 256], "Internal")
    ys = dt("ys", [256, D], "ExternalOutput")
    mask_f = dt("mask_f", [128, 128])
    mask_b = dt("mask_b", [128, 128])
    s5w = dt("s5w", [128, 128, 5, 128], "Internal")
    nsr = dt("nsr", [4, 2, 128, 64], "ExternalOutput")
    nsi = dt("nsi", [4, 2, 128, 64], "ExternalOutput")
    ident_in = dt("ident", [128, 128])
    yp = dt("yp", [4 * SEQ, D], "ExternalOutput")
    nk = dt("nk", [4 * SEQ, NH, 128], "ExternalOutput")
    nv = dt("nv", [4 * SEQ, NH, 128], "ExternalOutput")

    es = ExitStack()
    with es:
        kb = KB(nc, es)
        sb = lambda name, shape, dty=F32: es.enter_context(nc.sbuf_tensor(name, shape, dty))
        xT = sb("xT", [128, NCH, NT])
        hT = sb("hT", [128, NCH, NT], BF16)
        AW = 22528
        arena = sb("arena", [128, AW])

        def carve(off, shape, dty=F32, parts=(0, 128)):
            n = int(np.prod(shape))
            words = n if dty == F32 else (n + 1) // 2
            assert off + words <= AW, (off, shape)
            ap = arena[parts[0]:parts[1], off:off + words]
            if dty != F32:
                ap = ap.bitcast(dty)
            if len(shape) > 1:
                names = [f"d{i}" for i in range(len(shape))]
                pat = "p (" + " ".join(names) + ") -> p " + " ".join(names)
                ap = ap.rearrange(pat, **{nm: int(v) for nm, v in zip(names, shape)})
            return ap

        oT = carve(0, [NCH, NT], BF16)
        qT = [carve(4096 + 256 * i, [NT], BF16) for i in range(2)]
        kT = [carve(4608 + 256 * i, [NT], BF16) for i in range(2)]
        kf = [carve(5120 + 512 * i, [NT]) for i in range(2)]
        vf = [carve(6144 + 512 * i, [NT]) for i in range(2)]
        ktok = [carve(7168 + 512 * i, [4, 128]) for i in range(2)]
        vtok = [carve(8192 + 512 * i, [4, 128]) for i in range(2)]
        vtb = [carve(9216 + 256 * i, [4, 128], BF16) for i in range(2)]
        ET = [carve(9728 + 256 * i, [2, 256], BF16) for i in range(2)]
        rden = [carve(10240 + 256 * i, [256]) for i in range(2)]
        actT = carve(0, [22, NT], BF16)
        zc = [carve(5632 + 512 * i, [NT]) for i in range(4)]
        xtok = [sb(f"xtok{i}", [128, D]) for i in range(2)]
        NRING = 4
        wring = [sb(f"wr{i}", [128, 4096], BF16) for i in range(NRING)]
        wsem = [kb.newsem(f"W{i}") for i in range(NRING)]
        sq = [sb(f"sq{i}", [128, NT], BF16) for i in range(2)]
        rstd = sb("rstd", [128, NT])
        tmp = [sb(f"tmp{i}", [128, NT]) if i != 2 else None for i in range(4)]
        ident = sb("ident_sb", [128, 128])
        ones_bf = sb("ones_bf", [128, 128], BF16)
        condT = sb("condT", [128, NCH, 2])
        scT = sb("scT", [128, NCH, 2], BF16)
        modv = sb("modv", [128, 2, 96, 2])
        adab = sb("adab", [128, 2, 96])
        g1 = sb("g1", [128, 2, NCH])
        g2 = sb("g2", [128, 2, NCH])
        gfin = sb("gfin", [128, NCH])
        gs = sb("gs", [128, 2, 2, NCH, 2])
        cw = sb("cw", [128, 2, 3, 86])
        cb = sb("cb", [128, 2, 86])
        ytok = xtok
        a8 = sb("a8", [128, 2, 128])
        PS = [es.enter_context(nc.psum_tensor(f"ps{i}", [128, 512], F32)) for i in range(8)]
        psem = kb.newsem("PARAM")
        xsem = [kb.newsem(f"X{i}") for i in range(2)]
        osem = [kb.newsem(f"O{i}") for i in range(2)]
        ksem = [kb.newsem(f"KO{i}") for i in range(2)]
        vsem = [kb.newsem(f"VO{i}") for i in range(2)]
        cm = sb("cm", [64, 2, 64])
        jf = sb("jf", [64, 64])
        Jb = sb("Jb", [64, 64], BF16)
        rowm = sb("rowm", [128, 6, 4])
        ohs = sb("ohs", [128, 24])
        oh1 = ohs[:, 0:16]
        oh2 = ohs[:, 16:24]
        hb = sb("hb", [128, NCH, 2])
        hg = sb("hg", [128, NCORES, 32])
        hacc = sb("hacc", [128, 2, NCH])
        csem = kb.newsem("CTX")
        btsem = kb.newsem("BTS")
        cvsem = kb.newsem("CTXV")
        hsem = kb.newsem("HALO")
        ccsem = kb.newsem("CC")
        blsem = [kb.newsem(f"BL{i}") for i in range(2)]
        wstsem = [kb.newsem(f"WST{i}") for i in range(2)]
        gwsem = kb.newsem("GW")
        stsem = [kb.newsem(f"STO{i}") for i in range(2)]
        final_sems = set()

        def dma(eng, out, in_, reads, writes, sem):
            return kb.op(eng, lambda e: e.dma_start(out=out, in_=in_), reads=reads, writes=writes, dma_sem=sem)

        def act(out, in_, func, reads, writes, bias=None, scale=None):
            kw = {}
            if bias is not None:
                kw["bias"] = bias
            if scale is not None:
                kw["scale"] = scale
            return kb.op("act", lambda e: e.activation(out=out, in_=in_, func=func, **kw), reads=reads, writes=writes)

        def mm(out, lhsT, rhs, start, stop, reads, writes):
            return kb.op("pe", lambda e: e.matmul(out, lhsT, rhs, start=start, stop=stop), reads=reads, writes=writes)

        ring_i = [0]

        def load_slab(src_ap, shape):
            s = ring_i[0] % NRING
            ring_i[0] += 1
            n = shape[0] * shape[1]
            view = wring[s][:, 0:n].rearrange("p (a b) -> p a b", a=shape[0])
            dma("pool", view, src_ap, reads=[], writes=[("w", s)], sem=wsem[s])
            return view, ("w", s)

        evac_i = [0]

        def evac_copy(out, in_, reads, writes):
            evac_i[0] += 1
            if evac_i[0] % 2:
                return kb.op("dve", lambda e: e.tensor_copy(out, in_), reads=reads, writes=writes)
            return act(out, in_, AF.Copy, reads, writes)

        def pload(out, in_, key):
            dma("sp", out, in_, reads=[], writes=[key], sem=psem)

        with nc.allow_non_contiguous_dma(reason="small parameter loads"):
            pload(ident[:], ident_in, "ident")
            for n_ in range(2):
                pload(condT[:, :, n_], cond[n_].rearrange("(c p) -> p c", p=128), "condT")
            for l in range(2):
                pload(adab[:, l, :], ada_b[l].rearrange("(c p) -> p c", p=128), "adab")
            pload(g1[:], norm1_g.rearrange("l (c p) -> p l c", p=128), "g1")
            pload(g2[:], norm2_g.rearrange("l (c p) -> p l c", p=128), "g2")
            pload(gfin[:], final_g.rearrange("(c p) -> p c", p=128), "gfin")
            for l in range(2):
                for k3 in range(3):
                    pload(cw[:, l, k3, :], ffn_cw[l, k3].rearrange("(c p) -> p c", p=128), "cw")
            pload(cb[:], ffn_cb.rearrange("l (c p) -> p l c", p=128), "cb")
        pload(cm[:], cm_in, "cm")
        pload(jf[:], j_in, "jf")
        pload(rowm[:], rowm_in.rearrange("p (c i) -> p c i", c=6), "rowm")
        pload(ohs[:], oh_in, "oh1")
        kb.barrier()
        kb.op("dve", lambda e: e.tensor_copy(Jb[:], jf[:]), reads=["jf"], writes=["Jb"])
        kb.op("dve", lambda e: e.memset(ones_bf[:], 1.0), writes=["ones"])
        act(scT[:], condT[:], AF.Silu, ["condT"], ["scT"])

        for l in range(2):
            wv = ada_w[l].rearrange("(kc p) n -> p kc n", p=128)
            for s_ in range(48):
                slab, wkey = load_slab(wv[:, :, s_ * 256:(s_ + 1) * 256], [16, 256])
                pb = PS[s_ % 2]
                for j in range(2):
                    for kc in range(NCH):
                        mm(pb[:, 2 * j:2 * j + 2], slab[:, kc, j * 128:(j + 1) * 128], scT[:, kc, :],
                           kc == 0, kc == NCH - 1, [wkey, "scT"], [("ps", s_ % 2)])
                for j in range(2):
                    ch = 2 * s_ + j
                    kb.op("dve", lambda e, ch=ch, j=j, pb=pb, l=l: e.tensor_scalar(
                        out=modv[:, l, ch, :], in0=pb[:, 2 * j:2 * j + 2], scalar1=adab[:, l, ch:ch + 1],
                        scalar2=None, op0=ALU.add), reads=[("ps", s_ % 2), "adab"], writes=[("modv", l, ch)])
        for l in range(2):
            for sub, (gt, base) in enumerate(((g1, 16), (g2, 64))):
                for c in range(NCH):
                    kb.op("dve", lambda e, l=l, sub=sub, gt=gt, base=base, c=c: e.tensor_scalar(
                        out=gs[:, l, sub, c, :], in0=modv[:, l, base + c, :], scalar1=1.0,
                        scalar2=gt[:, l, c:c + 1], op0=ALU.add, op1=ALU.mult),
                        reads=[("modv", l, base + c), "g1", "g2"], writes=[("gs", l, sub, c)])

        def shift_ap(l, sub, c, grp):
            base = 0 if sub == 0 else 48
            return modv[:, l, base + c, grp:grp + 1], ("modv", l, base + c)

        def gate_ap(l, sub, c, grp):
            base = 32 if sub == 0 else 80
            return modv[:, l, base + c, grp:grp + 1], ("modv", l, base + c)

        def load_x(src_rows, ntok):
            for tt in range(ntok // 128):
                s = tt % 2
                dma("sp", xtok[s][:], src_rows[tt * 128:(tt + 1) * 128, :], reads=[], writes=[("xtok", s)], sem=xsem[s])
                for c4 in range(4):
                    for c in range(4):
                        ch = c4 * 4 + c
                        kb.op("pe", lambda e, ch=ch, c=c, s=s: e.transpose(
                            PS[6][:, c * 128:(c + 1) * 128], xtok[s][:, ch * 128:(ch + 1) * 128], ident[:]),
                            reads=[("xtok", s), "ident"], writes=[("ps", 6)])
                    evac_copy(xT[:, c4 * 4:(c4 + 1) * 4, tt * 128:(tt + 1) * 128],
                              PS[6][:].rearrange("p (c t) -> p c t", c=4), [("ps", 6)],
                              [("xT", c4 * 4 + c) for c in range(4)])

        def rms_stats(ntok):
            for c in range(NCH):
                s = c % 2
                act(sq[s][:, 0:ntok], xT[:, c, 0:ntok], AF.Square, [("xT", c)], [("sq", s)])
                mm(PS[7][:, 0:ntok], ones_bf[:], sq[s][:, 0:ntok], c == 0, c == NCH - 1,
                   [("sq", s), "ones"], [("ps", 7)])
            kb.op("dve", lambda e: e.tensor_scalar(out=tmp[3][:, 0:ntok], in0=PS[7][:, 0:ntok], scalar1=1.0 / D,
                                                   scalar2=EPS, op0=ALU.mult, op1=ALU.add),
                  reads=[("ps", 7)], writes=[("tmp", 3)])
            act(tmp[3][:, 0:ntok], tmp[3][:, 0:ntok], AF.Sqrt, [("tmp", 3)], [("tmp", 3)])
            kb.op("dve", lambda e: e.reciprocal(rstd[:, 0:ntok], tmp[3][:, 0:ntok]), reads=[("tmp", 3)], writes=["rstd"])

        def norm_mod(l, sub, grp, ntok, dst=None, dkey="hT"):
            dst = hT if dst is None else dst
            rms_stats(ntok)
            for c in range(NCH):
                s = c % 2
                kb.op("dve", lambda e, c=c, s=s: e.scalar_tensor_tensor(
                    out=tmp[s][:, 0:ntok], in0=xT[:, c, 0:ntok], scalar=gs[:, l, sub, c, grp:grp + 1],
                    in1=rstd[:, 0:ntok], op0=ALU.mult, op1=ALU.mult),
                    reads=[("xT", c), ("gs", l, sub, c), "rstd"], writes=[("tmp", s)])
                sap, skey = shift_ap(l, sub, c, grp)
                act(dst[:, c, 0:ntok], tmp[s][:, 0:ntok], AF.Identity, [("tmp", s), skey], [(dkey, c)], bias=sap)

        def proj(wmat, in_tile, in_key, nk_chunks, ntok, evac):
            wv = wmat.rearrange("(kc p) n -> p kc n", p=128)
            kgroups = [(k0, min(k0 + 16, nk_chunks)) for k0 in range(0, nk_chunks, 16)]
            for mp in range(8):
                slabs = []
                for (k0, k1) in kgroups:
                    slabs.append(load_slab(wv[:, k0:k1, mp * 256:(mp + 1) * 256], [k1 - k0, 256]))
                for j in range(2):
                    m = mp * 2 + j
                    pb = m % 2
                    for gi, (k0, k1) in enumerate(kgroups):
                        slab, wkey = slabs[gi]
                        for kc in range(k0, k1):
                            mm(PS[pb][:, 0:ntok], slab[:, kc - k0, j * 128:(j + 1) * 128], in_tile[:, kc, 0:ntok],
                               kc == 0, kc == nk_chunks - 1, [wkey, (in_key, kc)], [("ps", pb)])
                    evac(m, pb)

        def proj_residual(wmat, in_tile, in_key, nk_chunks, l, sub, grp, ntok):
            def evac(m, pb):
                gap, gkey = gate_ap(l, sub, m, grp)
                kb.op("dve", lambda e: e.scalar_tensor_tensor(
                    out=xT[:, m, 0:ntok], in0=PS[pb][:, 0:ntok], scalar=gap, in1=xT[:, m, 0:ntok],
                    op0=ALU.mult, op1=ALU.add), reads=[("ps", pb), gkey, ("xT", m)], writes=[("xT", m)])
            proj(wmat, in_tile, in_key, nk_chunks, ntok, evac)

        def attention_prompt(p):
            kb.barrier()
            wv = w_qkv.rearrange("(kc p) n -> p kc n", p=128)
            nk_v = nk.rearrange("(tt p) h d -> p tt h d", p=128)
            nv_v = nv.rearrange("(tt p) h d -> p tt h d", p=128)
            for h in range(NH):
                b = h % 2
                for which in range(3):
                    slab, wkey = load_slab(wv[:, :, which * D + h * 128: which * D + (h + 1) * 128], [16, 128])
                    pb = which
                    for kc in range(NCH):
                        mm(PS[pb][:], slab[:, kc, :], hT[:, kc, :], kc == 0, kc == NCH - 1,
                           [wkey, ("hT", kc)], [("ps", pb)])
                    if which == 0:
                        act(qT[b][:], PS[0][:], AF.Copy, [("ps", 0)], [("qT", b)])
                    elif which == 1:
                        kb.op("dve", lambda e, b=b: e.tensor_copy(kf[b][:], PS[1][:]), reads=[("ps", 1)], writes=[("kf", b)])
                        kb.op("pool", lambda e, b=b: e.tensor_copy(kT[b][:], kf[b][:]), reads=[("kf", b)], writes=[("kT", b)])
                    else:
                        act(vf[b][:], PS[2][:], AF.Copy, [("ps", 2)], [("vf", b)])
                for src, dst, dkey, osm, outv in ((kf, ktok, "ktok", ksem, nk_v), (vf, vtok, "vtok", vsem, nv_v)):
                    for tt in range(4):
                        kb.op("pe", lambda e, tt=tt, src=src, b=b: e.transpose(
                            PS[6][:, tt * 128:(tt + 1) * 128], src[b][:, tt * 128:(tt + 1) * 128], ident[:]),
                            reads=[("kf" if src is kf else "vf", b), "ident"], writes=[("ps", 6)])
                    kb.op("dve", lambda e, dst=dst, b=b: e.tensor_copy(
                        dst[b][:], PS[6][:].rearrange("p (t d) -> p t d", t=4)), reads=[("ps", 6)], writes=[(dkey, b)])
                    dma("sp", outv[:, 4 * p:4 * p + 4, h, :], dst[b][:], reads=[(dkey, b)], writes=[], sem=osm[b])
                    final_sems.add(osm[b])
                kb.op("pool", lambda e, b=b: e.tensor_copy(vtb[b][:], vtok[b][:]), reads=[("vtok", b)], writes=[("vtb", b)])
                for sq_ in range(2):
                    eb = sq_
                    q_ap = qT[b][:, sq_ * 256:(sq_ + 1) * 256]
                    for kcn in range(2):
                        mm(PS[3 + sq_][:, kcn * 256:(kcn + 1) * 256],
                           kT[b][:, sq_ * 256 + kcn * 128: sq_ * 256 + (kcn + 1) * 128], q_ap, True, True,
                           [("kT", b), ("qT", b)], [("ps", 3 + sq_)])
                    act(ET[eb][:], PS[3 + sq_][:].rearrange("p (k q) -> p k q", k=2), AF.Exp,
                        [("ps", 3 + sq_)], [("ET", eb)], scale=ATT_SCALE)
                    for kcn in range(2):
                        mm(PS[5][:, 0:256], vtb[b][:, sq_ * 2 + kcn, :], ET[eb][:, kcn, :], kcn == 0, kcn == 1,
                           [("vtb", b), ("ET", eb)], [("ps", 5)])
                    for kcn in range(2):
                        mm(PS[5][:, 256:512], ones_bf[:], ET[eb][:, kcn, :], kcn == 0, kcn == 1,
                           ["ones", ("ET", eb)], [("ps", 5)])
                    kb.op("dve", lambda e, eb=eb: e.reciprocal(rden[eb][:], PS[5][:, 256:512]),
                          reads=[("ps", 5)], writes=[("rden", eb)])
                    kb.op("dve", lambda e, eb=eb, h=h, sq_=sq_: e.tensor_tensor(
                        out=oT[:, h, sq_ * 256:(sq_ + 1) * 256], in0=PS[5][:, 0:256], in1=rden[eb][:], op=ALU.mult),
                        reads=[("ps", 5), ("rden", eb)], writes=[("oT", h)])

        def ffn(l, grp, ntok, seglen, halo=False):
            kb.barrier()
            ncol = ntok + (2 if halo else 0)
            upv = ffn_up[l].rearrange("(kc p) n -> p kc n", p=128)
            nseg = ntok // seglen
            for half, (f0, f1) in enumerate(((0, 22), (22, FCH))):
                for f in range(f0, f1):
                    for gv in range(2):
                        col = gv * FF + f * 128
                        cidx = gv * FCH + f
                        slab, wkey = load_slab(upv[:, :, col:col + 128], [16, 128])
                        pb = gv
                        for kc in range(NCH):
                            mm(PS[pb][:, 0:ncol], slab[:, kc, :], hT[:, kc, 0:ncol], kc == 0, kc == NCH - 1,
                               [wkey, ("hT", kc)], [("ps", pb)])
                        z = zc[gv * 2 + (f % 2)]
                        zkey = ("zc", gv * 2 + (f % 2))
                        act(z[:, 0:ntok], PS[pb][:, 0:ntok], AF.Identity, [("ps", pb), "cw", "cb"], [zkey],
                            bias=cb[:, l, cidx:cidx + 1], scale=cw[:, l, 1, cidx:cidx + 1])
                        zv = z[:, 0:ntok].rearrange("p (s t) -> p s t", s=nseg)
                        pv = PS[pb][:, 0:ntok].rearrange("p (s t) -> p s t", s=nseg)
                        kb.op("dve", lambda e, zv=zv, pv=pv, cidx=cidx: e.scalar_tensor_tensor(
                            out=zv[:, :, 1:seglen], in0=pv[:, :, 0:seglen - 1], scalar=cw[:, l, 0, cidx:cidx + 1],
                            in1=zv[:, :, 1:seglen], op0=ALU.mult, op1=ALU.add), reads=[("ps", pb), zkey, "cw"], writes=[zkey])
                        kb.op("dve", lambda e, zv=zv, pv=pv, cidx=cidx: e.scalar_tensor_tensor(
                            out=zv[:, :, 0:seglen - 1], in0=pv[:, :, 1:seglen], scalar=cw[:, l, 2, cidx:cidx + 1],
                            in1=zv[:, :, 0:seglen - 1], op0=ALU.mult, op1=ALU.add), reads=[("ps", pb), zkey, "cw"], writes=[zkey])
                        if halo:
                            for tcol, hcol, k3 in ((0, ntok, 0), (ntok - 1, ntok + 1, 2)):
                                kb.op("dve", lambda e, z=z, pb=pb, cidx=cidx, tcol=tcol, hcol=hcol, k3=k3: e.scalar_tensor_tensor(
                                    out=z[:, tcol:tcol + 1], in0=PS[pb][:, hcol:hcol + 1], scalar=cw[:, l, k3, cidx:cidx + 1],
                                    in1=z[:, tcol:tcol + 1], op0=ALU.mult, op1=ALU.add),
                                    reads=[("ps", pb), zkey, "cw"], writes=[zkey])
                    zg = zc[0 + (f % 2)]
                    zvv = zc[2 + (f % 2)]
                    act(zg[:, 0:ntok], zg[:, 0:ntok], AF.Silu, [("zc", f % 2)], [("zc", f % 2)])
                    kb.op("pool", lambda e, zg=zg, zvv=zvv, f=f, f0=f0: e.tensor_tensor(
                        out=actT[:, f - f0, 0:ntok], in0=zg[:, 0:ntok], in1=zvv[:, 0:ntok], op=ALU.mult),
                        reads=[("zc", f % 2), ("zc", 2 + f % 2)], writes=[("actT", f - f0)])
                proj_residual(ffn_dn[l][f0 * 128:f1 * 128, :], actT, "actT", f1 - f0, l, 1, grp, ntok)

        def final_out(dst_rows, ntok):
            rms_stats(ntok)
            for tt in range(ntok // 128):
                s = tt % 2
                for c in range(NCH):
                    ts_ = c % 2
                    kb.op("dve", lambda e, c=c, ts_=ts_, tt=tt: e.scalar_tensor_tensor(
                        out=tmp[ts_][:, 0:128], in0=xT[:, c, tt * 128:(tt + 1) * 128], scalar=gfin[:, c:c + 1],
                        in1=rstd[:, tt * 128:(tt + 1) * 128], op0=ALU.mult, op1=ALU.mult),
                        reads=[("xT", c), "gfin", "rstd"], writes=[("tmp", ts_)])
                    kb.op("pe", lambda e, c=c, ts_=ts_: e.transpose(
                        PS[6][:, (c % 4) * 128:(c % 4 + 1) * 128], tmp[ts_][:, 0:128], ident[:]),
                        reads=[("tmp", ts_), "ident"], writes=[("ps", 6)])
                    if c % 4 == 3:
                        evac_copy(ytok[s][:, (c - 3) * 128:(c + 1) * 128], PS[6][:], [("ps", 6)], [("xtok", s)])
                dma("sp", dst_rows[tt * 128:(tt + 1) * 128, :], ytok[s][:], reads=[("xtok", s)], writes=[], sem=osem[s])
                final_sems.add(osem[s])

        TWO_PI = 6.283185307179586

        def s5_generate():
            kb.barrier()
            tn = ["are", "aim", "ldt", "dt", "lr", "th", "mag", "fr", "kf", "f", "m", "gq", "sinv", "cosv", "abr", "abi",
                  "den", "rdn", "nr", "cfr", "cfi", "m2", "aivr", "aivi", "rr", "ri", "qr", "qi", "r8r", "r8i",
                  "q8r", "q8i", "SCr", "SCi", "YCr", "YCi", "t1", "t2", "dcol", "Mf", "Mb", "wt2",
                  "BBr", "BBi", "CTr", "CTi", "u1", "u2"]
            T = {nm: carve(8192 + 128 * i, [128]) for i, nm in enumerate(tn)}
            o = 8192 + 128 * len(tn)
            ki = carve(o, [128]).bitcast(mybir.dt.int32); o += 128
            braw = [[carve(o + 256 * sl + 128 * r, [8, 16]) for r in range(2)] for sl in range(2)]; o += 512
            wst = [carve(o + 384 * i, [3, 128]) for i in range(2)]; o += 768
            Cn = [carve(2048 * r, [16, 2, 64]) for r in range(2)]
            PX = [carve(4096 + 1024 * r, [128, 8]) for r in range(2)]
            PZ = [carve(6144 + 1024 * r, [128, 8]) for r in range(2)]
            xflat = xT[:].rearrange("p c t -> p (c t)")
            blk = lambda i: xflat[:, 1024 * i:1024 * (i + 1)].rearrange("p (g n) -> p g n", g=8)
            Xre, Xim, Ximn, Zre, Zim, WSr, WSi, WYr = [blk(i) for i in range(8)]
            hflat = hT[:].rearrange("p c t -> p (c t)").bitcast(F32)
            blk2 = lambda i: hflat[:, 1024 * i:1024 * (i + 1)].rearrange("p (g n) -> p g n", g=8)
            WYin, T1, T2 = [blk2(i) for i in range(3)]
            v4 = lambda a: a.rearrange("p g (s c) -> p g s c", s=8)

            def tt(out, a, b, op, eng="dve", key="tab"):
                kb.op(eng, lambda e: e.tensor_tensor(out=out, in0=a, in1=b, op=op), reads=[key], writes=[key])

            def ts(out, a, s1, op0, s2=None, op1=None, key="tab"):
                if op1 is None:
                    kb.op("dve", lambda e: e.tensor_scalar(out=out, in0=a, scalar1=s1, scalar2=None, op0=op0),
                          reads=[key], writes=[key])
                else:
                    kb.op("dve", lambda e: e.tensor_scalar(out=out, in0=a, scalar1=s1, scalar2=s2, op0=op0, op1=op1),
                          reads=[key], writes=[key])

            def cp(out, a, key="tab"):
                kb.op("dve", lambda e: e.tensor_copy(out, a), reads=[key], writes=[key])

            def af(out, a, func, scale=None):
                act(out, a, func, ["tab"], ["tab"], scale=scale)

            def cmul(orr, oi, ar, ai, br, bi, t1, t2, key="tab"):
                tt(t1, ar, br, ALU.mult, key=key); tt(t2, ai, bi, ALU.mult, key=key); tt(orr, t1, t2, ALU.subtract, key=key)
                tt(t1, ar, bi, ALU.mult, key=key); tt(t2, ai, br, ALU.mult, key=key); tt(oi, t1, t2, ALU.add, key=key)

            for d in range(2):
                hp_ = slice(64 * d, 64 * d + 64)
                pload(T["are"][hp_, :], s5_a_re[d].rearrange("g p -> p g"), "tab")
                pload(T["aim"][hp_, :], s5_a_im[d].rearrange("g p -> p g"), "tab")
                pload(T["ldt"][hp_, :], s5_log_dt[d].partition_broadcast(64), "tab")
                for r, src in enumerate((s5_c_re, s5_c_im)):
                    pload(Cn[r][:, :, d, :], src[d].rearrange("(gb g8) c p -> (g8 c) gb p", g8=8), "tab")
            for s_ in range(8):
                pload(T["dcol"][16 * s_:16 * s_ + 16, :], s5_d.rearrange("(g c) -> c g", c=16), "tab")
            pload(T["Mf"], mask_f, "tab")
            pload(T["Mb"], mask_b, "tab")
            af(T["dt"], T["ldt"], AF.Exp)
            tt(T["lr"], T["are"], T["dt"], ALU.mult); tt(T["th"], T["aim"], T["dt"], ALU.mult)
            af(T["mag"], T["lr"], AF.Exp)
            ts(T["fr"], T["th"], 1.0 / TWO_PI, ALU.mult)
            cp(ki, T["fr"]); cp(T["kf"], ki)
            tt(T["f"], T["fr"], T["kf"], ALU.subtract)
            ts(T["m"], T["f"], 0.5, ALU.is_gt); tt(T["f"], T["f"], T["m"], ALU.subtract)
            ts(T["m"], T["f"], -0.5, ALU.is_lt); tt(T["f"], T["f"], T["m"], ALU.add)
            ts(T["gq"], T["f"], 0.25, ALU.add)
            ts(T["m"], T["gq"], 0.5, ALU.is_gt); tt(T["gq"], T["gq"], T["m"], ALU.subtract)
            af(T["sinv"], T["f"], AF.Sin, scale=TWO_PI)
            af(T["cosv"], T["gq"], AF.Sin, scale=TWO_PI)
            tt(T["abr"], T["mag"], T["cosv"], ALU.mult); tt(T["abi"], T["mag"], T["sinv"], ALU.mult)
            tt(T["t1"], T["are"], T["are"], ALU.mult); tt(T["t2"], T["aim"], T["aim"], ALU.mult)
            tt(T["den"], T["t1"], T["t2"], ALU.add)
            kb.op("dve", lambda e: e.reciprocal(T["rdn"], T["den"]), reads=["tab"], writes=["tab"])
            ts(T["nr"], T["abr"], -1.0, ALU.add)
            tt(T["t1"], T["nr"], T["are"], ALU.mult); tt(T["t2"], T["abi"], T["aim"], ALU.mult)
            tt(T["t1"], T["t1"], T["t2"], ALU.add); tt(T["cfr"], T["t1"], T["rdn"], ALU.mult)
            tt(T["t1"], T["abi"], T["are"], ALU.mult); tt(T["t2"], T["nr"], T["aim"], ALU.mult)
            tt(T["t1"], T["t1"], T["t2"], ALU.subtract); tt(T["cfi"], T["t1"], T["rdn"], ALU.mult)
            tt(T["m2"], T["mag"], T["mag"], ALU.mult)
            kb.op("dve", lambda e: e.reciprocal(T["m2"], T["m2"]), reads=["tab"], writes=["tab"])
            tt(T["aivr"], T["abr"], T["m2"], ALU.mult); tt(T["aivi"], T["abi"], T["m2"], ALU.mult)
            ts(T["aivi"], T["aivi"], -1.0, ALU.mult)
            lo, hi = slice(0, 64), slice(64, 128)
            cp(T["rr"][lo], T["aivr"][lo]); cp(T["ri"][lo], T["aivi"][lo]); cp(T["rr"][hi], T["abr"][hi]); cp(T["ri"][hi], T["abi"][hi])
            cp(T["qr"][lo], T["abr"][lo]); cp(T["qi"][lo], T["abi"][lo]); cp(T["qr"][hi], T["aivr"][hi]); cp(T["qi"][hi], T["aivi"][hi])
            for P_, br, bi, o8r, o8i in ((PX, "rr", "ri", "r8r", "r8i"), (PZ, "qr", "qi", "q8r", "q8i")):
                kb.op("dve", lambda e, P_=P_: e.memset(P_[0][:, :, 0], 1.0), reads=["tab"], writes=["tab"])
                kb.op("dve", lambda e, P_=P_: e.memset(P_[1][:, :, 0], 0.0), reads=["tab"], writes=["tab"])
                for s_ in range(7):
                    cmul(P_[0][:, :, s_ + 1], P_[1][:, :, s_ + 1], P_[0][:, :, s_], P_[1][:, :, s_], T[br], T[bi], T["t1"], T["t2"])
                cmul(T[o8r], T[o8i], P_[0][:, :, 7], P_[1][:, :, 7], T[br], T[bi], T["t1"], T["t2"])
            cp(T["SCr"][lo], PZ[0][lo, :, 7]); cp(T["SCi"][lo], PZ[1][lo, :, 7])
            kb.op("dve", lambda e: e.memset(T["SCr"][hi], 1.0), reads=["tab"], writes=["tab"])
            kb.op("dve", lambda e: e.memset(T["SCi"][hi], 0.0), reads=["tab"], writes=["tab"])
            cp(T["YCr"][lo], T["qr"][lo]); cp(T["YCi"][lo], T["qi"][lo]); cp(T["YCr"][hi], T["r8r"][hi]); cp(T["YCi"][hi], T["r8i"][hi])
            cp(a8[lo, 0, :], T["q8r"][lo]); cp(a8[lo, 1, :], T["q8i"][lo]); cp(a8[hi, 0, :], T["r8r"][hi]); cp(a8[hi, 1, :], T["r8i"][hi])

            bc3 = lambda a: a.unsqueeze(2).to_broadcast([128, 8, 16])
            for gb in range(16):
                gsl = slice(gb * 8, gb * 8 + 8)
                sl = gb % 2
                for d in range(2):
                    for r, src in enumerate((s5_b_re, s5_b_im)):
                        dma("sp", braw[sl][r][64 * d:64 * d + 64], src[d, gb * 8:gb * 8 + 8].rearrange("g p c -> p g c"),
                            reads=[], writes=[("braw", sl)], sem=blsem[sl])
                k_ = "blk"
                rk = lambda *a: list(a)
                def ttb(out, a, b, op, extra=()):
                    kb.op("dve", lambda e: e.tensor_tensor(out=out, in0=a, in1=b, op=op),
                          reads=["blk", "tab"] + list(extra), writes=["blk"])
                u1 = T["u1"].rearrange("p (g c) -> p g c", g=8); u2 = T["u2"].rearrange("p (g c) -> p g c", g=8)
                BBr = T["BBr"].rearrange("p (g c) -> p g c", g=8); BBi = T["BBi"].rearrange("p (g c) -> p g c", g=8)
                CTr = T["CTr"].rearrange("p (g c) -> p g c", g=8); CTi = T["CTi"].rearrange("p (g c) -> p g c", g=8)
                crb, cib = bc3(T["cfr"][:, gsl]), bc3(T["cfi"][:, gsl])
                bk = [("braw", sl)]
                ttb(u1, crb, braw[sl][0], ALU.mult, bk); ttb(u2, cib, braw[sl][1], ALU.mult, bk); ttb(BBr, u1, u2, ALU.subtract)
                ttb(u1, crb, braw[sl][1], ALU.mult, bk); ttb(u2, cib, braw[sl][0], ALU.mult, bk); ttb(BBi, u1, u2, ALU.add)
                for r, CT_ in enumerate((CTr, CTi)):
                    kb.op("pe", lambda e, r=r, gb=gb: e.transpose(
                        PS[6][:, 128 * r:128 * (r + 1)], Cn[r][:, gb, :, :].rearrange("p d q -> p (d q)"), ident[:]),
                        reads=["tab", "ident"], writes=[("ps", 6)])
                    kb.op("dve", lambda e, r=r, CT_=CT_: e.tensor_copy(
                        CT_, PS[6][:, 128 * r:128 * (r + 1)].rearrange("p (g c) -> p g c", g=8)),
                        reads=[("ps", 6), "blk"], writes=["blk"])
                b4 = [128, 8, 8, 16]
                pw = lambda P_, r: P_[r][:, gsl, :].unsqueeze(3).to_broadcast(b4)
                vc = lambda a: a.unsqueeze(2).to_broadcast(b4)
                def cblk(orr, oi, pr, pi, vr, vi):
                    ttb(v4(T1), pr, vr, ALU.mult); ttb(v4(T2), pi, vi, ALU.mult); ttb(orr, T1, T2, ALU.subtract)
                    ttb(v4(T1), pr, vi, ALU.mult); ttb(v4(T2), pi, vr, ALU.mult); ttb(oi, T1, T2, ALU.add)
                cblk(Xre, Xim, pw(PX, 0), pw(PX, 1), vc(BBr), vc(BBi))
                kb.op("dve", lambda e: e.tensor_scalar(out=Ximn, in0=Xim, scalar1=-1.0, scalar2=None, op0=ALU.mult),
                      reads=["blk"], writes=["blk"])
                cblk(Zre, Zim, pw(PZ, 0), pw(PZ, 1), vc(CTr), vc(CTi))
                sc = lambda nm: T[nm][:, gsl].unsqueeze(2).to_broadcast([128, 8, 128])
                ttb(T1, sc("SCr"), Xre, ALU.mult); ttb(T2, sc("SCi"), Xim, ALU.mult); ttb(WSr, T1, T2, ALU.subtract)
                ttb(T1, sc("SCr"), Xim, ALU.mult); ttb(T2, sc("SCi"), Xre, ALU.mult); ttb(WSi, T1, T2, ALU.add)
                ttb(T1, sc("YCr"), Zre, ALU.mult); ttb(T2, sc("YCi"), Zim, ALU.mult); ttb(WYr, T1, T2, ALU.subtract)
                ttb(T1, sc("YCr"), Zim, ALU.mult); ttb(T2, sc("YCi"), Zre, ALU.mult)
                kb.op("dve", lambda e: e.scalar_tensor_tensor(out=WYin, in0=T1, scalar=-1.0, in1=T2, op0=ALU.mult,
                                                              op1=ALU.subtract), reads=["blk"], writes=["blk"])
                dma("sp", s5w[gb * 8:gb * 8 + 8, :, 3, :].rearrange("g p n -> p g n"), WYr, reads=["blk"], writes=[], sem=gwsem)
                dma("sp", s5w[gb * 8:gb * 8 + 8, :, 4, :].rearrange("g p n -> p g n"), WYin, reads=["blk"], writes=[], sem=gwsem)
                for g8 in range(8):
                    g = gb * 8 + g8
                    ws = g % 2
                    for half, pb in ((lo, 0), (hi, 1)):
                        kb.op("pe", lambda e, half=half, pb=pb, g8=g8: e.matmul(
                            PS[pb][:, 0:128], Xre[half, g8, :], Zre[half, g8, :], start=True, stop=False),
                            reads=["blk"], writes=[("ps", pb)])
                        kb.op("pe", lambda e, half=half, pb=pb, g8=g8: e.matmul(
                            PS[pb][:, 0:128], Ximn[half, g8, :], Zim[half, g8, :], start=False, stop=True),
                            reads=["blk"], writes=[("ps", pb)])
                    kb.op("dve", lambda e, ws=ws: e.tensor_tensor(out=wst[ws][:, 2, :], in0=PS[0][:, 0:128], in1=T["Mf"], op=ALU.mult),
                          reads=[("ps", 0), "tab"], writes=[("wst", ws)])
                    kb.op("dve", lambda e: e.tensor_tensor(out=T["wt2"], in0=PS[1][:, 0:128], in1=T["Mb"], op=ALU.mult),
                          reads=[("ps", 1), "tab"], writes=["wt2"])
                    kb.op("dve", lambda e, ws=ws: e.tensor_tensor(out=wst[ws][:, 2, :], in0=wst[ws][:, 2, :], in1=T["wt2"], op=ALU.add),
                          reads=[("wst", ws), "wt2"], writes=[("wst", ws)])
                    kb.op("dve", lambda e, ws=ws, g=g: e.scalar_tensor_tensor(
                        out=wst[ws][:, 2, :], in0=ident[:], scalar=T["dcol"][:, g:g + 1], in1=wst[ws][:, 2, :],
                        op0=ALU.mult, op1=ALU.add), reads=[("wst", ws), "ident", "tab"], writes=[("wst", ws)])
                    for r, W_ in enumerate((WSr, WSi)):
                        kb.op("pe", lambda e, r=r, W_=W_, g8=g8: e.transpose(PS[2 + r][:, 0:128], W_[:, g8, :], ident[:]),
                              reads=["blk", "ident"], writes=[("ps", 2 + r)])
                        evac_copy(wst[ws][:, r, :], PS[2 + r][:, 0:128], [("ps", 2 + r), ("wst", ws)], [("wst", ws)])
                    dma("sp", s5w[g, :, 0:3, :], wst[ws][:], reads=[("wst", ws)], writes=[], sem=wstsem[ws])
            kb.barrier()

        def s5_pass(l, grp, ntok, nseq, ns_dst, carry=False):
            NC = ntok // 8
            KS = NC // nseq
            norm_mod(l, 0, grp, ntok)
            kb.barrier()
            if not carry:
                OFF = dict(Ug=0, S=4096, Hp=8192, Yg=10240, Utok=11264, Ytok=13312, yT=15360, Hst=19456, rt=19584,
                           Hfin=19840, gA=20352, stt=21376)
            else:
                OFF = dict(Ug=0, S=2048, Hp=10240, Yg=11264, Utok=11776, Ytok=13824, yT=15872, Hst=17920, rt=17984,
                           Hfin=18112, Hin=18368, gA=18624, G=19136, ct=21184, pw=21696)
            Ug = carve(OFF["Ug"], [128, NC], BF16)
            nS = 4 if carry else 1
            Sb = [carve(OFF["S"] + 64 * NC * i, [2, 32, NC]) for i in range(nS)]
            Hp = carve(OFF["Hp"], [2, 32, NC], BF16)
            Yg = [carve(OFF["Yg"] + 8 * NC * i, [8, NC]) for i in range(2)]
            Utok = [carve(OFF["Utok"] + 1024 * i, [8, 8, 16], parts=(0, 64)) for i in range(2)]
            Ytok = [carve(OFF["Ytok"] + 1024 * i, [8, 128], parts=(0, 64)) for i in range(2)]
            yT = carve(OFF["yT"], [NCH, ntok], BF16)
            Hst = carve(OFF["Hst"], [2, 32, nseq])
            rt = [carve(OFF["rt"] + 32 * nseq * i, [32, nseq]) for i in range(4)]
            Hfin = carve(OFF["Hfin"], [2, 128, nseq])
            gA = [carve(OFF["gA"] + 8 * NC * i, [8, NC]) for i in range(2)]
            lo, hi = slice(0, 64), slice(64, 128)
            winv = s5_w_in.rearrange("(kc p) n -> p kc n", p=128)
            for m in range(NCH):
                slab, wkey = load_slab(winv[:, :, m * 128:(m + 1) * 128], [16, 128])
                ub = m % 2
                for s_ in range(8):
                    pb = s_ // 4
                    for kc in range(NCH):
                        mm(PS[pb][0:NC, (s_ % 4) * 128:(s_ % 4 + 1) * 128], hT[:, kc, s_:ntok:8], slab[:, kc, :],
                           kc == 0, kc == NCH - 1, [wkey, ("hT", kc)], [("ps", pb)])
                for pb in range(2):
                    evac_copy(Utok[ub][0:NC, :, 4 * pb:4 * pb + 4, :].rearrange("p g s c -> p s g c"),
                              PS[pb][0:NC, :].rearrange("p (s g c) -> p s g c", s=4, g=8),
                              [("ps", pb)], [("utok", ub, pb)])
                for g8 in range(8):
                    kb.op("pe", lambda e, g8=g8, ub=ub: e.transpose(
                        PS[2][:, g8 * NC:(g8 + 1) * NC], Utok[ub][0:NC, g8].rearrange("p s c -> p (s c)"), ident[0:NC, 0:NC]),
                        reads=[("utok", ub, 0), ("utok", ub, 1), "ident"], writes=[("ps", 2)])
                evac_copy(Ug[:, m * 8:(m + 1) * 8, :], PS[2][:, 0:8 * NC].rearrange("p (g k) -> p g k", g=8),
                          [("ps", 2)], [("ug", m)])
            s5v = lambda g0, j0, j1: s5w[g0:g0 + 8, :, j0:j1, :].rearrange("g p j n -> p g (j n)")

            def s_matmuls(blk, S, si):
                g0b = blk * 32
                for gg in range(4):
                    slab, wkey = load_slab(s5v(g0b + gg * 8, 0, 2), [8, 256])
                    for g8 in range(8):
                        g = g0b + gg * 8 + g8
                        for r in range(2):
                            mm(PS[3 + r][:, g8 * NC:(g8 + 1) * NC], slab[:, g8, 128 * r:128 * (r + 1)], Ug[:, g, :], True, True,
                               [wkey, ("ug", g // 8)], [("ps", 3 + r)])
                    for r in range(2):
                        evac_copy(S[:, r, gg * 8:gg * 8 + 8, :], PS[3 + r][:, 0:8 * NC].rearrange("p (g k) -> p g k", g=8),
                                  [("ps", 3 + r)], [("S", si, r, gg)])

            def recur(blk, S, si, init, write_hp, fin):
                g0b = blk * 32
                Sv = S.rearrange("p r g (s k) -> p r g s k", s=nseq)
                Hpv = Hp.rearrange("p r g (s k) -> p r g s k", s=nseq)
                skeys = [("S", si, r, gg) for r in range(2) for gg in range(4)]

                def recur_half(half, eng, hk):
                    if init is None:
                        kb.op(eng, lambda e: e.memset(Hst[half], 0.0), reads=[], writes=[("hst", hk)])
                    else:
                        kb.op(eng, lambda e: e.tensor_copy(Hst[half], init[half, :, g0b:g0b + 32, :]),
                              reads=["hin"], writes=[("hst", hk)])
                    are_b = a8[half, 0, g0b:g0b + 32].unsqueeze(2).to_broadcast([64, 32, nseq])
                    aim_b = a8[half, 1, g0b:g0b + 32].unsqueeze(2).to_broadcast([64, 32, nseq])
                    Hr, Hi = Hst[half, 0], Hst[half, 1]
                    t1, t2, t3, t4 = [rt[i][half] for i in range(4)]

                    def o(fn, reads, writes):
                        kb.op(eng, fn, reads=reads, writes=writes)

                    def step(kc_):
                        if write_hp:
                            o(lambda e: e.tensor_copy(Hpv[half, :, :, :, kc_], Hst[half]), [("hst", hk)], [("hp", hk)])
                        o(lambda e: e.tensor_tensor(out=t1, in0=Hr, in1=are_b, op=ALU.mult), [("hst", hk)], [("rt", hk)])
                        o(lambda e: e.tensor_tensor(out=t2, in0=Hi, in1=aim_b, op=ALU.mult), [("hst", hk)], [("rt", hk)])
                        o(lambda e: e.tensor_tensor(out=t3, in0=Hr, in1=aim_b, op=ALU.mult), [("hst", hk)], [("rt", hk)])
                        o(lambda e: e.tensor_tensor(out=t4, in0=Hi, in1=are_b, op=ALU.mult), [("hst", hk)], [("rt", hk)])
                        o(lambda e: e.tensor_tensor(out=Hr, in0=t1, in1=t2, op=ALU.subtract), [("rt", hk)], [("hst", hk)])
                        o(lambda e: e.tensor_tensor(out=Hr, in0=Hr, in1=Sv[half, 0, :, :, kc_], op=ALU.add),
                          [("hst", hk)] + skeys, [("hst", hk)])
                        o(lambda e: e.tensor_tensor(out=Hi, in0=t3, in1=t4, op=ALU.add), [("rt", hk)], [("hst", hk)])
                        o(lambda e: e.tensor_tensor(out=Hi, in0=Hi, in1=Sv[half, 1, :, :, kc_], op=ALU.add),
                          [("hst", hk)] + skeys, [("hst", hk)])
                    for k in range(KS):
                        step(k if hk == 0 else KS - 1 - k)
                    if fin is not None:
                        kb.op(eng, lambda e: e.tensor_copy(fin[half, :, g0b:g0b + 32, :], Hst[half]),
                              reads=[("hst", hk)], writes=[("hfin", hk)])
                recur_half(lo, "dve", 0)
                recur_half(hi, "pool", 1)

            def y_part(blk):
                g0b = blk * 32
                for gg in range(4):
                    m = blk * 4 + gg
                    yb = m % 2
                    slab, wkey = load_slab(s5v(g0b + gg * 8, 2, 5), [8, 384])
                    for g8 in range(8):
                        g = g0b + gg * 8 + g8
                        gl = gg * 8 + g8
                        oap = PS[5][:, g8 * NC:(g8 + 1) * NC]
                        mm(oap, slab[:, g8, 0:128], Ug[:, g, :], True, False, [wkey, ("ug", g // 8)], [("ps", 5)])
                        mm(oap, slab[:, g8, 128:256], Hp[:, 0, gl, :], False, False, [wkey, ("hp", 0), ("hp", 1)], [("ps", 5)])
                        mm(oap, slab[:, g8, 256:384], Hp[:, 1, gl, :], False, True, [wkey, ("hp", 0), ("hp", 1)], [("ps", 5)])
                    p5 = PS[5][:, 0:8 * NC].rearrange("p (g k) -> p g k", g=8)
                    act(gA[yb], p5, AF.Square, [("ps", 5)], [("gA", yb)])
                    kb.op("dve", lambda e, yb=yb: e.tensor_scalar(out=gA[yb], in0=gA[yb], scalar1=0.044715, scalar2=1.0,
                                                           op0=ALU.mult, op1=ALU.add), reads=[("gA", yb)], writes=[("gA", yb)])
                    kb.op("dve", lambda e, yb=yb, p5=p5: e.tensor_tensor(out=gA[yb], in0=gA[yb], in1=p5, op=ALU.mult),
                          reads=[("gA", yb), ("ps", 5)], writes=[("gA", yb)])
                    act(gA[yb], gA[yb], AF.Sigmoid, [("gA", yb)], [("gA", yb)], scale=1.5957691216057308)
                    kb.op("dve", lambda e, yb=yb, p5=p5: e.tensor_tensor(out=Yg[yb], in0=gA[yb], in1=p5, op=ALU.mult),
                          reads=[("gA", yb), ("ps", 5)], writes=[("yg", yb)])
                    for g8 in range(8):
                        pb = 6 + g8 // 4
                        kb.op("pe", lambda e, g8=g8, pb=pb, yb=yb: e.transpose(
                            PS[pb][0:NC, (g8 % 4) * 128:(g8 % 4 + 1) * 128], Yg[yb][:, g8, :], ident[:]),
                            reads=[("yg", yb), "ident"], writes=[("ps", pb)])
                    for q4 in range(2):
                        evac_copy(Ytok[yb][0:NC, :, q4 * 64:(q4 + 1) * 64].rearrange("p t (g c) -> p g t c", g=4),
                                  PS[6 + q4][0:NC, :].rearrange("p (g t c) -> p g t c", g=4, t=8),
                                  [("ps", 6 + q4)], [("ytok", yb, q4)])
                    for t_ in range(8):
                        kb.op("pe", lambda e, t_=t_, yb=yb: e.transpose(
                            PS[2][:, t_ * NC:(t_ + 1) * NC], Ytok[yb][0:NC, t_, :], ident[0:NC, 0:NC]),
                            reads=[("ytok", yb, 0), ("ytok", yb, 1), "ident"], writes=[("ps", 2)])
                    evac_copy(yT[:, m, :].rearrange("p (k t) -> p t k", t=8),
                              PS[2][:, 0:8 * NC].rearrange("p (t k) -> p t k", t=8), [("ps", 2)], [("yT", m)])

            if not carry:
                for blk in range(4):
                    s_matmuls(blk, Sb[0], 0)
                    recur(blk, Sb[0], 0, None, True, Hfin)
                    y_part(blk)
            else:
                for blk in range(4):
                    s_matmuls(blk, Sb[blk], blk)
                    recur(blk, Sb[blk], blk, None, False, Hfin)
                Hin = s5_carry(Hfin, OFF)
                for blk in range(4):
                    recur(blk, Sb[blk], blk, Hin, True, None)
                    y_part(blk)
            def gate_evac(m, pb):
                act(tmp[m % 2][:, 0:ntok], PS[pb][:, 0:ntok], AF.Sigmoid, [("ps", pb)], [("tmp", m % 2)])
                kb.op("dve", lambda e: e.tensor_tensor(out=hT[:, m, 0:ntok], in0=tmp[m % 2][:, 0:ntok], in1=yT[:, m, :],
                                                       op=ALU.mult), reads=[("tmp", m % 2), ("yT", m)], writes=[("hT", m)])
            proj(s5_w_gate, yT, "yT", NCH, ntok, gate_evac)
            proj_residual(s5_w_out, hT, "hT", NCH, l, 0, grp, ntok)
            if ns_dst is not None:
                stt = [carve(OFF["stt"] + 128 * i, [128]) for i in range(2)]
                for r in range(2):
                    for sq_ in range(nseq):
                        sb_ = (r * nseq + sq_) % 2
                        kb.op("pe", lambda e, r=r, sq_=sq_: e.transpose(PS[6][:, 0:128], Hfin[:, r, :, sq_], ident[:]),
                              reads=[("hfin", 0), ("hfin", 1), "ident"], writes=[("ps", 6)])
                        evac_copy(stt[sb_], PS[6][:, 0:128], [("ps", 6)], [("stt", sb_)])
                        dma("sp", ns_dst[r][sq_].rearrange("d g q -> g d q"), stt[sb_].rearrange("p (d q) -> p d q", d=2),
                            reads=[("stt", sb_)], writes=[], sem=stsem[sb_])
                        final_sems.add(stsem[sb_])

        def s5_carry(Fall, OFF):
            lo, hi = slice(0, 64), slice(64, 128)
            Hin = carve(OFF["Hin"], [2, 128, 1])
            G = carve(OFF["G"], [8, 2, 128])
            ct = [carve(OFF["ct"] + 128 * i, [128]) for i in range(4)]
            pw = [carve(OFF["pw"] + 256 * i, [2, 128]) for i in range(2)]
            HA = hT[:].rearrange("p c t -> p (c t)").bitcast(F32)[:, 0:2048].rearrange("p (s j r g) -> p s j r g", s=2, j=4, r=2)
            hflat_ = hT[:].rearrange("p c t -> p (c t)").bitcast(F32)
            h0t = hflat_[:, 2048:2560].rearrange("p (r s g) -> p r s g", r=2, s=2)
            for r_, src_ in enumerate(st_h0):
                for s_ in range(2):
                    for d_ in range(2):
                        dma("sp", h0t[64 * d_:64 * d_ + 64, r_, s_, :], src_[s_, d_].rearrange("g p -> p g"),
                            reads=[], writes=["h0t"] + [("hT", c) for c in range(NCH)], sem=hsem)
            dma("sp", st_in, Fall.rearrange("p r g o -> p (r g o)"), reads=[("hfin", 0), ("hfin", 1)], writes=["st_in"], sem=hsem)
            kb.op("pool", lambda e: e.collective_compute("AllGather", ALU.bypass, replica_groups=[list(range(NCORES))],
                                                         ins=[st_in], outs=[st_out]),
                  reads=["st_in"], writes=["st_out"], dma_sem=ccsem, inc=1)
            dma("sp", G.rearrange("p r i g -> p r (i g)"), st_out.rearrange("(r p) n -> p r n", p=128),
                reads=["st_out"], writes=["car"], sem=hsem)

            def dv(fn):
                kb.op("dve", fn, reads=["car", "h0t", "oh2"], writes=["car"])

            def cmul_acc(out_r, out_i, a_r, a_i, x_r, x_i, add_r, add_i, half):
                t1, t2 = ct[0][half], ct[1][half]
                dv(lambda e: e.tensor_tensor(out=t1, in0=a_r, in1=x_r, op=ALU.mult))
                dv(lambda e: e.tensor_tensor(out=t2, in0=a_i, in1=x_i, op=ALU.mult))
                dv(lambda e: e.tensor_tensor(out=t1, in0=t1, in1=t2, op=ALU.subtract))
                t3, t4 = ct[2][half], ct[3][half]
                dv(lambda e: e.tensor_tensor(out=t3, in0=a_r, in1=x_i, op=ALU.mult))
                dv(lambda e: e.tensor_tensor(out=t4, in0=a_i, in1=x_r, op=ALU.mult))
                dv(lambda e: e.tensor_tensor(out=t3, in0=t3, in1=t4, op=ALU.add))
                if add_r is None:
                    dv(lambda e: e.tensor_copy(out_r, t1))
                    dv(lambda e: e.tensor_copy(out_i, t3))
                else:
                    dv(lambda e: e.tensor_tensor(out=out_r, in0=t1, in1=add_r, op=ALU.add))
                    dv(lambda e: e.tensor_tensor(out=out_i, in0=t3, in1=add_i, op=ALU.add))
            full = slice(0, 128)
            cur_r, cur_i = a8[:, 0, :], a8[:, 1, :]
            for i in range(5):
                dst = pw[i % 2]
                cmul_acc(dst[:, 0, :], dst[:, 1, :], cur_r, cur_i, cur_r, cur_i, None, None, full)
                cur_r, cur_i = dst[:, 0, :], dst[:, 1, :]
            A = pw[0]
            for s_ in range(2):
                dv(lambda e, s_=s_: e.tensor_copy(HA[lo, s_, 0], h0t[lo, :, s_, :]))
                for j in range(1, 4):
                    cmul_acc(HA[lo, s_, j, 0], HA[lo, s_, j, 1], A[lo, 0, :], A[lo, 1, :], HA[lo, s_, j - 1, 0], HA[lo, s_, j - 1, 1],
                             G[lo, 4 * s_ + j - 1, 0, :], G[lo, 4 * s_ + j - 1, 1, :], lo)
                dv(lambda e, s_=s_: e.tensor_copy(HA[hi, s_, 3], h0t[hi, :, s_, :]))
                for j in range(2, -1, -1):
                    cmul_acc(HA[hi, s_, j, 0], HA[hi, s_, j, 1], A[hi, 0, :], A[hi, 1, :], HA[hi, s_, j + 1, 0], HA[hi, s_, j + 1, 1],
                             G[hi, 4 * s_ + j + 1, 0, :], G[hi, 4 * s_ + j + 1, 1, :], hi)
            Hv = Hin.rearrange("p r g o -> p r (g o)")
            dv(lambda e: e.memset(Hv, 0.0))
            for s_ in range(2):
                for j in range(4):
                    dv(lambda e, s_=s_, j=j: e.scalar_tensor_tensor(
                        out=Hv, in0=HA[:, s_, j], scalar=oh2[:, 4 * s_ + j:4 * s_ + j + 1], in1=Hv, op0=ALU.mult, op1=ALU.add))
            kb.op("dve", lambda e: e.tensor_copy(ct[0], ct[0]), reads=["car"], writes=["hin"])
            return Hin

        hW = carve(10752, [NCH, 768], BF16)
        BT = carve(16896, [15, 64], parts=(0, 64))
        BTb = carve(17856, [15, 64], BF16, parts=(0, 64))
        kTs = carve(18336, [768], BF16)
        vfs = carve(18720, [768])
        vtbs = carve(19488, [6, 128], BF16)
        kctok = carve(19872, [4, 128])
        kcT = carve(20384, [512], BF16)
        vcb = carve(20640, [4, 128], BF16)
        ETs = carve(20896, [10, 256], BF16)

        def attention_sample():
            kb.barrier()
            wv = w_qkv.rearrange("(kc p) n -> p kc n", p=128)
            ckv = cache_k_in.rearrange("(t p) h d -> p t h d", p=128)
            cvv = cache_v_in.rearrange("(t p) h d -> p t h d", p=128)
            for h in range(NH):
                b = h % 2
                slab, wkey = load_slab(wv[:, :, h * 128:(h + 1) * 128], [16, 128])
                for kc in range(NCH):
                    mm(PS[0][:, 0:256], slab[:, kc, :], hW[:, kc, 256:512], kc == 0, kc == NCH - 1, [wkey], [("ps", 0)])
                act(qT[b][:, 0:256], PS[0][:, 0:256], AF.Copy, [("ps", 0)], [("qT", b)])
                for which, (pa, pbk) in ((1, (1, 2)), (2, (3, 4))):
                    slab, wkey = load_slab(wv[:, :, which * D + h * 128: which * D + (h + 1) * 128], [16, 128])
                    for kc in range(NCH):
                        mm(PS[pa][:, 0:512], slab[:, kc, :], hW[:, kc, 0:512], kc == 0, kc == NCH - 1, [wkey], [("ps", pa)])
                        mm(PS[pbk][:, 0:256], slab[:, kc, :], hW[:, kc, 512:768], kc == 0, kc == NCH - 1, [wkey], [("ps", pbk)])
                    dst = kTs if which == 1 else vfs
                    dkey = "kTs" if which == 1 else "vfs"
                    evac_copy(dst[:, 0:512], PS[pa][:, 0:512], [("ps", pa)], [dkey])
                    evac_copy(dst[:, 512:768], PS[pbk][:, 0:256], [("ps", pbk)], [dkey])
                for half3 in range(2):
                    for t3 in range(3):
                        tt_ = half3 * 3 + t3
                        kb.op("pe", lambda e, tt_=tt_, t3=t3: e.transpose(
                            PS[7][:, t3 * 128:(t3 + 1) * 128], vfs[:, tt_ * 128:(tt_ + 1) * 128], ident[:]),
                            reads=["vfs", "ident"], writes=[("ps", 7)])
                    evac_copy(vtbs[:, half3 * 3:half3 * 3 + 3, :], PS[7][:, 0:384].rearrange("p (t d) -> p t d", t=3),
                              [("ps", 7)], ["vtbs"])
                dma("sp", kctok, ckv[:, :, h, :], reads=[], writes=["kctok"], sem=csem)
                dma("pool", vcb, cvv[:, :, h, :], reads=[], writes=["vcb"], sem=cvsem)
                for t4 in range(4):
                    kb.op("pe", lambda e, t4=t4: e.transpose(PS[7][:, t4 * 128:(t4 + 1) * 128], kctok[:, t4, :], ident[:]),
                          reads=["kctok", "ident"], writes=[("ps", 7)])
                evac_copy(kcT, PS[7][:, :], [("ps", 7)], ["kcT"])
                dma("sp", BT, bass.AP(tensor=rpbp_t, offset=64 + 465 * h - 48, ap=[[1, 64], [31, 15], [1, 64]]),
                    reads=[], writes=["BT"], sem=btsem)
                kb.op("dve", lambda e: e.tensor_tensor(out=BT, in0=BT, in1=cm[:, 0, :].unsqueeze(1).to_broadcast([64, 15, 64]),
                                                       op=ALU.mult), reads=["BT", "cm"], writes=["BT"])
                kb.op("dve", lambda e: e.tensor_tensor(out=BTb, in0=BT, in1=cm[:, 1, :].unsqueeze(1).to_broadcast([64, 15, 64]),
                                                       op=ALU.add), reads=["BT", "cm"], writes=["BTb"])
                for c in range(10):
                    if c < 6:
                        mm(PS[5][:, 0:256], kTs[:, c * 128:(c + 1) * 128], qT[b][:, 0:256], True, False,
                           ["kTs", ("qT", b)], [("ps", 5)])
                        for i in range(4):
                            dr = 2 * c + 3 - i
                            mm(PS[5][:, 64 * i:64 * i + 64], BTb[:, dr:dr + 2, :].rearrange("p a k -> p (a k)"), Jb[:, :],
                               False, i == 3, ["BTb", "Jb"], [("ps", 5)])
                        for i in range(4):
                            act(ETs[:, c, 64 * i:64 * i + 64], PS[5][:, 64 * i:64 * i + 64], AF.Exp, [("ps", 5), "rowm"],
                                [("ETs", c)], bias=rowm[:, c, i:i + 1], scale=ATT_SCALE)
                    else:
                        mm(PS[5][:, 0:256], kcT[:, (c - 6) * 128:(c - 5) * 128], qT[b][:, 0:256], True, True,
                           ["kcT", ("qT", b)], [("ps", 5)])
                        act(ETs[:, c, :], PS[5][:, 0:256], AF.Exp, [("ps", 5)], [("ETs", c)], scale=ATT_SCALE)
                for c in range(10):
                    lhs = vtbs[:, c, :] if c < 6 else vcb[:, c - 6, :]
                    mm(PS[6][:, 0:256], lhs, ETs[:, c, :], c == 0, c == 9, ["vtbs", "vcb", ("ETs", c)], [("ps", 6)])
                for c in range(10):
                    mm(PS[6][:, 256:512], ones_bf[:], ETs[:, c, :], c == 0, c == 9, ["ones", ("ETs", c)], [("ps", 6)])
                kb.op("dve", lambda e, b=b: e.reciprocal(rden[b][:], PS[6][:, 256:512]), reads=[("ps", 6)], writes=[("rden", b)])
                kb.op("dve", lambda e, b=b, h=h: e.tensor_tensor(out=oT[:, h, 0:256], in0=PS[6][:, 0:256], in1=rden[b][:],
                                                                  op=ALU.mult), reads=[("ps", 6), ("rden", b)], writes=[("oT", h)])

        def halo_exchange(li):
            kb.op("dve", lambda e: e.tensor_copy(hb[:, :, 0], hT[:, :, 0]), reads=[("hT", c) for c in range(NCH)], writes=["hb"])
            kb.op("dve", lambda e: e.tensor_copy(hb[:, :, 1], hT[:, :, 255]), reads=[("hT", c) for c in range(NCH)], writes=["hb"])
            dma("sp", halo_in[li], hb[:].rearrange("p c e -> p (c e)"), reads=["hb"], writes=["halo_in"], sem=hsem)
            kb.op("pool", lambda e: e.collective_compute("AllGather", ALU.bypass, replica_groups=[list(range(NCORES))],
                                                         ins=[halo_in[li]], outs=[halo_out[li]]),
                  reads=["halo_in"], writes=["halo_out"], dma_sem=ccsem, inc=1)
            dma("sp", hg[:], halo_out[li].rearrange("(r p) n -> p r n", p=128), reads=["halo_out"], writes=["hg"], sem=hsem)
            hgv = hg[:].rearrange("p r (c e) -> p r c e", e=2)
            for which, e_idx, col in ((0, 1, 256), (1, 0, 257)):
                kb.op("dve", lambda e, which=which: e.memset(hacc[:, which, :], 0.0), reads=[], writes=["hacc"])
                for r in range(NCORES):
                    kb.op("dve", lambda e, which=which, e_idx=e_idx, r=r: e.scalar_tensor_tensor(
                        out=hacc[:, which, :], in0=hgv[:, r, :, e_idx], scalar=oh1[:, which * 8 + r:which * 8 + r + 1],
                        in1=hacc[:, which, :], op0=ALU.mult, op1=ALU.add), reads=["hg", "hacc", "oh1"], writes=["hacc"])
                kb.op("dve", lambda e, which=which, col=col: e.tensor_copy(hT[:, :, col], hacc[:, which, :]),
                      reads=["hacc"], writes=[("hT", c) for c in range(NCH)])

        def sample_pass():
            kb.barrier()
            for piece in (0, 2, 1):
                load_x(xsw[piece * 256:(piece + 1) * 256, :], 256)
                norm_mod(0, 0, 1, 256, dst=hW[:, :, piece * 256:(piece + 1) * 256], dkey="hW%d" % piece)
            attention_sample()
            proj_residual(w_o, oT, "oT", NCH, 0, 0, 1, 256)
            norm_mod(0, 1, 1, 256)
            halo_exchange(0)
            ffn(0, 1, 256, 256, halo=True)
            s5_pass(1, 1, 256, 1, None, carry=True)
            norm_mod(1, 1, 1, 256)
            halo_exchange(1)
            ffn(1, 1, 256, 256, halo=True)
            final_out(ys, 256)

        s5_generate()
        for p in range(2):
            load_x(xp[p * NT:(p + 1) * NT, :], NT)
            norm_mod(0, 0, 0, NT)
            attention_prompt(p)
            proj_residual(w_o, oT, "oT", NCH, 0, 0, 0, NT)
            norm_mod(0, 1, 0, NT)
            ffn(0, 0, NT, SEQ)
            s5_pass(1, 0, NT, 2, [[nsr[2 * p + q] for q in range(2)], [nsi[2 * p + q] for q in range(2)]])
            norm_mod(1, 1, 0, NT)
            ffn(1, 0, NT, SEQ)
            final_out(yp[p * NT:(p + 1) * NT, :], NT)
        sample_pass()

        with nc.allow_non_contiguous_dma(reason="small parameter loads"), nc.Block() as block:
            kb.emit(block, [(s, kb.cnt[s]) for s in sorted(final_sems)])
    return nc


_NC = None


def kernel(**inputs):
    global _NC
    f = lambda k: np.ascontiguousarray(np.asarray(inputs[k], dtype=np.float32))
    x_prompt = f("x_prompt")
    x_sample = f("x_sample")
    c = f("c")
    c_ctx = f("c_ctx")
    if _NC is None:
        _NC = build_program()
    shared = {
        "ada_w": f("ada_w"), "ada_b": f("ada_b"), "norm1_g": f("norm1_g"), "norm2_g": f("norm2_g"),
        "na_w_qkv": f("na_w_qkv")[0], "na_w_o": f("na_w_o")[0],
        "ffn_w_up": f("ffn_w_up"), "ffn_conv_w": f("ffn_conv_w"), "ffn_conv_b": f("ffn_conv_b"),
        "ffn_w_down": f("ffn_w_down"), "final_g": f("final_g"), "ident": np.eye(128, dtype=np.float32),
        "s5_w_in": f("s5_w_in")[0], "s5_w_gate": f("s5_w_gate")[0], "s5_w_out": f("s5_w_out")[0],
        "s5_a_re": f("s5_a_re")[0], "s5_a_im": f("s5_a_im")[0], "s5_log_dt": f("s5_log_dt")[0],
        "s5_b_re": f("s5_b_re")[0], "s5_b_im": f("s5_b_im")[0], "s5_c_re": f("s5_c_re")[0], "s5_c_im": f("s5_c_im")[0],
        "s5_d": f("s5_d")[0],
        "mask_f": (np.arange(128)[None, :] // 16 >= np.arange(128)[:, None] // 16).astype(np.float32),
        "mask_b": (np.arange(128)[:, None] // 16 >= np.arange(128)[None, :] // 16).astype(np.float32),
    }
    qc = 63 - np.arange(64)[:, None]
    kc_ = np.arange(64)[None, :]
    cstart = np.clip(qc - 8, 0, 48)
    col_in = ((kc_ >= cstart) & (kc_ < cstart + 16)).astype(np.float32)
    cm_in = np.stack([col_in * np.float32(128.0 ** 0.5), (1.0 - col_in) * np.float32(-1e30)], 1).astype(np.float32)
    j_in = (np.arange(64)[:, None] == 63 - np.arange(64)[None, :]).astype(np.float32)
    rpbp = np.zeros(7568, np.float32)
    rpbp[64:64 + 7440] = f("na_rpb").reshape(-1)
    cache_k, cache_v = f("cache_k"), f("cache_v")
    state_re, state_im = f("state_re"), f("state_im")
    in_maps = []
    for core in range(NCORES):
        b, j = core // 4, core % 4
        m = dict(shared)
        m["xp"] = x_prompt[4 * core:4 * core + 4].reshape(4 * SEQ, D)
        m["cond"] = np.stack([c_ctx, c[b]], 0)
        xw = np.zeros((12, 64, D), np.float32)
        g0, g1 = max(4 * j - 4, 0), min(4 * j + 8, 16)
        xw[g0 - (4 * j - 4):g1 - (4 * j - 4)] = x_sample[b].reshape(16, 64, D)[g0:g1]
        m["xsw"] = xw.reshape(768, D)
        m["cache_k"] = cache_k[b, 0]
        m["cache_v"] = cache_v[b, 0]
        m["rpbp"] = rpbp
        m["state_re"] = state_re[:, 0]
        m["state_im"] = state_im[:, 0]
        m["cm_in"] = cm_in
        m["j_in"] = j_in
        rowm = np.zeros((128, 6, 4), np.float32)
        for cch in range(6):
            for i in range(4):
                r = 4 * j + i
                st = min(max(r - 4, 0), 8)
                for jr in range(2):
                    kr = 4 * j - 4 + 2 * cch + jr
                    if not (st <= kr < st + 8):
                        rowm[64 * jr:64 * jr + 64, cch, i] = -1e30
        m["rowm_in"] = rowm.reshape(128, 24)
        oh = np.zeros((128, 24), np.float32)
        if j > 0:
            oh[:, core - 1] = 1.0
        if j < 3:
            oh[:, 8 + core + 1] = 1.0
        oh[:, 16 + core] = 1.0
        m["oh_in"] = oh
        in_maps.append(m)
    res = run_bass_kernel_spmd(_NC, in_maps, core_ids=list(range(NCORES)))
    r = res.results
    y_prompt = np.concatenate([r[i]["yp"].reshape(4, SEQ, D) for i in range(NCORES)], 0)
    new_k = np.concatenate([r[i]["nk"].reshape(4, 1, SEQ, NH, 128) for i in range(NCORES)], 0)
    new_v = np.concatenate([r[i]["nv"].reshape(4, 1, SEQ, NH, 128) for i in range(NCORES)], 0)
    y_sample = np.stack([np.concatenate([r[4 * b_ + j_]["ys"] for j_ in range(4)], 0) for b_ in range(2)], 0)
    nsr = np.concatenate([r[i]["nsr"].reshape(4, 1, 2, 128, 64) for i in range(NCORES)], 0)
    nsi = np.concatenate([r[i]["nsi"].reshape(4, 1, 2, 128, 64) for i in range(NCORES)], 0)
    return (y_prompt, y_sample, new_k, new_v, nsr, nsi)
```
